# Optimizing a Trainium2 kernel written in Bass

```python
import math
import jax, jax.numpy as jnp
from jax import lax
import numpy as np

D_MODEL = 1024
BATCH = 8
SEQ = 4096
DEPTH = 2

PLE_DIM = 256
HEAD_DIM = 128
MIX_WIDTH = D_MODEL
N_HEADS_A = MIX_WIDTH // (2 * HEAD_DIM)
A_QK_DIM = HEAD_DIM // 2
A_V_DIM = HEAD_DIM
N_HEADS_B = MIX_WIDTH // (2 * HEAD_DIM)
N_HEADS_C = MIX_WIDTH // HEAD_DIM
ROPE_THETA = 500000.0
ROPE_FRACTION = 4
QBLK = 128
DILATED_PATTERNS = ((128, 1), (512, 4), (2048, 16))
FORGET_BIAS_INIT = 2.0
RMS_EPS = 1e-6
N_EVEN = (DEPTH + 1) // 2
N_ODD = DEPTH // 2
A_QK_WIDTH = N_HEADS_A * 2 * A_QK_DIM
A_V_WIDTH = N_HEADS_A * A_V_DIM
B_WIDTH = N_HEADS_B * HEAD_DIM
EVEN_IN = 2 * A_QK_WIDTH + A_V_WIDTH + 3 * B_WIDTH + N_HEADS_B + MIX_WIDTH
ODD_IN = 4 * MIX_WIDTH

kernel_name = 'hybrid_diff_fox_dilated_block'


def rms_norm(x, g):
    x32 = x.astype(jnp.float32)
    y = x32 * lax.rsqrt(jnp.mean(x32 * x32, axis=-1, keepdims=True) + RMS_EPS)
    return (y * g.astype(jnp.float32)).astype(x.dtype)


def partial_rope(x, positions):
    rot = x.shape[-1] // ROPE_FRACTION
    half = rot // 2
    inv_freq = jnp.exp(-math.log(ROPE_THETA) * jnp.arange(half, dtype=jnp.float32) / half)
    ang = positions.astype(jnp.float32)[..., None] * inv_freq
    ang = ang.reshape(ang.shape[:2] + (1,) * (x.ndim - 3) + (half,))
    cos, sin = jnp.cos(ang), jnp.sin(ang)
    x1 = x[..., :half].astype(jnp.float32)
    x2 = x[..., half:rot].astype(jnp.float32)
    out = jnp.concatenate([x1 * cos - x2 * sin, x2 * cos + x1 * sin,
                           x[..., rot:].astype(jnp.float32)], axis=-1)
    return out.astype(x.dtype)


def to_qblocks(t):
    b, s = t.shape[:2]
    return jnp.moveaxis(t.reshape((b, s // QBLK, QBLK) + t.shape[2:]), 1, 0)


def from_qblocks(t):
    nb, b = t.shape[:2]
    return jnp.moveaxis(t, 0, 1).reshape((b, nb * QBLK) + t.shape[3:])


def diff_attention(q1, q2, k1, k2, v, lam):
    s_len = k1.shape[1]
    kpos = jnp.arange(s_len)

    def one_block(args):
        q1b, q2b, t0 = args
        causal = (t0 + jnp.arange(QBLK))[:, None] >= kpos[None, :]

        def probs(qb, k):
            s = jnp.einsum('bqhd,bkhd->bhqk', qb, k).astype(jnp.float32)
            return jax.nn.softmax(jnp.where(causal, s, -jnp.inf), axis=-1)

        w = probs(q1b, k1) - lam * probs(q2b, k2)
        return jnp.einsum('bhqk,bkhd->bqhd', w.astype(v.dtype), v)

    t0s = jnp.arange(s_len // QBLK, dtype=jnp.int32) * QBLK
    return from_qblocks(lax.map(one_block, (to_qblocks(q1), to_qblocks(q2), t0s)))


def fox_attention(q, k, v, cum_logf):
    s_len = k.shape[1]
    kpos = jnp.arange(s_len)
    c_k = jnp.swapaxes(cum_logf, 1, 2)

    def one_block(args):
        qb, cqb, t0 = args
        causal = (t0 + jnp.arange(QBLK))[:, None] >= kpos[None, :]
        s = jnp.einsum('bqhd,bkhd->bhqk', qb, k).astype(jnp.float32)
        s = s + jnp.swapaxes(cqb, 1, 2)[..., None] - c_k[:, :, None, :]
        pr = jax.nn.softmax(jnp.where(causal, s, -jnp.inf), axis=-1)
        return jnp.einsum('bhqk,bkhd->bqhd', pr.astype(v.dtype), v)

    t0s = jnp.arange(s_len // QBLK, dtype=jnp.int32) * QBLK
    return from_qblocks(lax.map(one_block, (to_qblocks(q), to_qblocks(cum_logf), t0s)))


def dilated_window_attention(q, k, v, window, dil):
    b, s_len, h, d = q.shape
    steps = window // dil
    blk = steps
    span = blk * dil
    s_pad = -(-s_len // span) * span
    pad = ((0, 0), (0, s_pad - s_len), (0, 0), (0, 0))
    q, k, v = jnp.pad(q, pad), jnp.pad(k, pad), jnp.pad(v, pad)
    nb = s_pad // span

    def split(t):
        return t.reshape(b, nb, blk, dil, h, d)

    def with_prev(t):
        prev = jnp.pad(t[:, :-1], ((0, 0), (1, 0), (0, 0), (0, 0), (0, 0), (0, 0)))
        return jnp.concatenate([prev, t], axis=2)

    qb = split(q)
    kk, vv = with_prev(split(k)), with_prev(split(v))
    s = jnp.einsum('bnqrhd,bnkrhd->bnrhqk', qb, kk).astype(jnp.float32)
    dist = jnp.arange(blk)[:, None] + blk - jnp.arange(2 * blk)[None, :]
    band = (dist >= 0) & (dist <= steps)
    exists = (jnp.arange(nb)[:, None, None] > 0) | (jnp.arange(2 * blk)[None, None, :] >= blk)
    valid = band[None] & exists
    s = jnp.where(valid[None, :, None, None], s, -jnp.inf)
    lse = jax.nn.logsumexp(s, axis=-1)
    pr = jnp.exp(s - lse[..., None])
    o = jnp.einsum('bnrhqk,bnkrhd->bnqrhd', pr.astype(v.dtype), vv)
    o = o.reshape(b, s_pad, h, d)[:, :s_len]
    lse = jnp.transpose(lse, (0, 1, 4, 2, 3)).reshape(b, s_pad, h)[:, :s_len]
    return o, lse


def even_layer(h, positions, norm_g, w_in, b_forget, qn_a, kn_a, qn_b, kn_b,
               lam_q1, lam_k1, lam_q2, lam_k2, subln_g, w_out, lam_init):
    b, s_len, _ = h.shape
    hn = rms_norm(h, norm_g)
    proj = hn @ w_in
    sizes = (A_QK_WIDTH, A_QK_WIDTH, A_V_WIDTH, B_WIDTH, B_WIDTH, B_WIDTH, N_HEADS_B)
    cuts, acc = [], 0
    for sz in sizes:
        acc += sz
        cuts.append(acc)
    qa, ka, va, qb, kb, vb, f_logit, z = jnp.split(proj, cuts, axis=-1)

    qa = partial_rope(rms_norm(qa.reshape(b, s_len, N_HEADS_A, 2, A_QK_DIM), qn_a), positions)
    qa = qa * (A_QK_DIM ** -0.5)
    ka = partial_rope(rms_norm(ka.reshape(b, s_len, N_HEADS_A, 2, A_QK_DIM), kn_a), positions)
    lam = (jnp.exp(jnp.sum(lam_q1.astype(jnp.float32) * lam_k1.astype(jnp.float32)))
           - jnp.exp(jnp.sum(lam_q2.astype(jnp.float32) * lam_k2.astype(jnp.float32)))
           + lam_init)
    oa = diff_attention(qa[..., 0, :], qa[..., 1, :], ka[..., 0, :], ka[..., 1, :],
                        va.reshape(b, s_len, N_HEADS_A, A_V_DIM), lam)
    oa = rms_norm(oa, subln_g) * (1.0 - lam_init)

    qb = rms_norm(qb.reshape(b, s_len, N_HEADS_B, HEAD_DIM), qn_b) * (HEAD_DIM ** -0.5)
    kb = rms_norm(kb.reshape(b, s_len, N_HEADS_B, HEAD_DIM), kn_b)
    cum_logf = jnp.cumsum(jax.nn.log_sigmoid((f_logit + b_forget).astype(jnp.float32)), axis=1)
    ob = fox_attention(qb, kb, vb.reshape(b, s_len, N_HEADS_B, HEAD_DIM), cum_logf)

    mixed = jnp.concatenate([oa.reshape(b, s_len, A_V_WIDTH), ob.reshape(b, s_len, B_WIDTH)],
                            axis=-1) * jax.nn.silu(z)
    return h + mixed @ w_out


def odd_layer(h, positions, norm_g, w_in, qn_c, kn_c, w_out):
    b, s_len, _ = h.shape
    hn = rms_norm(h, norm_g)
    proj = hn @ w_in
    q, k, v, z = jnp.split(proj, [MIX_WIDTH, 2 * MIX_WIDTH, 3 * MIX_WIDTH], axis=-1)
    q = partial_rope(rms_norm(q.reshape(b, s_len, N_HEADS_C, HEAD_DIM), qn_c), positions)
    q = q * (HEAD_DIM ** -0.5)
    k = partial_rope(rms_norm(k.reshape(b, s_len, N_HEADS_C, HEAD_DIM), kn_c), positions)
    v = v.reshape(b, s_len, N_HEADS_C, HEAD_DIM)
    outs, lses = [], []
    for window, dil in DILATED_PATTERNS:
        o_i, lse_i = dilated_window_attention(q, k, v, window, dil)
        outs.append(o_i)
        lses.append(lse_i)
    wts = jax.nn.softmax(jnp.stack(lses), axis=0)
    o = jnp.sum(wts[..., None] * jnp.stack(outs).astype(jnp.float32), axis=0).astype(h.dtype)
    mixed = o.reshape(b, s_len, MIX_WIDTH) * jax.nn.silu(z)
    return h + mixed @ w_out


def per_layer_embedding(h, p_i, w_ple, w_gate):
    return h + (p_i @ w_ple) * jax.nn.sigmoid(h @ w_gate)


def setup_inputs(seed: int = 0) -> dict:
    key = jax.random.key(seed)
    ks = jax.random.split(key, 24)

    def nrm(k, shape, scale):
        return scale * jax.random.normal(k, shape, jnp.float32)

    def gain(k, shape):
        return 1.0 + 0.02 * jax.random.normal(k, shape, jnp.float32)

    positions = (jnp.arange(SEQ, dtype=jnp.int32)[None, :]
                 + jax.random.randint(ks[2], (BATCH, 1), 0, 1024, dtype=jnp.int32))
    return {
        'x': nrm(ks[0], (BATCH, SEQ, D_MODEL), 1.0),
        'p': nrm(ks[1], (DEPTH, BATCH, SEQ, PLE_DIM), 1.0),
        'positions': positions,
        'norm_g': gain(ks[3], (DEPTH, D_MODEL)),
        'w_in_even': nrm(ks[4], (N_EVEN, D_MODEL, EVEN_IN), D_MODEL ** -0.5),
        'b_forget': FORGET_BIAS_INIT + 0.1 * jax.random.normal(ks[5], (N_EVEN, N_HEADS_B), jnp.float32),
        'qn_a': gain(ks[6], (N_EVEN, A_QK_DIM)),
        'kn_a': gain(ks[7], (N_EVEN, A_QK_DIM)),
        'qn_b': gain(ks[8], (N_EVEN, HEAD_DIM)),
        'kn_b': gain(ks[9], (N_EVEN, HEAD_DIM)),
        'lam_q1': nrm(ks[10], (N_EVEN, A_QK_DIM), 0.1),
        'lam_k1': nrm(ks[11], (N_EVEN, A_QK_DIM), 0.1),
        'lam_q2': nrm(ks[12], (N_EVEN, A_QK_DIM), 0.1),
        'lam_k2': nrm(ks[13], (N_EVEN, A_QK_DIM), 0.1),
        'subln_g': gain(ks[14], (N_EVEN, A_V_DIM)),
        'w_out_even': nrm(ks[15], (N_EVEN, MIX_WIDTH, D_MODEL), MIX_WIDTH ** -0.5),
        'w_in_odd': nrm(ks[16], (N_ODD, D_MODEL, ODD_IN), D_MODEL ** -0.5),
        'qn_c': gain(ks[17], (N_ODD, HEAD_DIM)),
        'kn_c': gain(ks[18], (N_ODD, HEAD_DIM)),
        'w_out_odd': nrm(ks[19], (N_ODD, MIX_WIDTH, D_MODEL), MIX_WIDTH ** -0.5),
        'w_ple': nrm(ks[20], (DEPTH, PLE_DIM, D_MODEL), PLE_DIM ** -0.5),
        'w_ple_gate': nrm(ks[21], (DEPTH, D_MODEL, D_MODEL), D_MODEL ** -0.5),
    }


def reference(x, p, positions, norm_g, w_in_even, b_forget, qn_a, kn_a, qn_b, kn_b,
              lam_q1, lam_k1, lam_q2, lam_k2, subln_g, w_out_even,
              w_in_odd, qn_c, kn_c, w_out_odd, w_ple, w_ple_gate):
    h = x
    for i in range(DEPTH):
        if i % 2 == 0:
            e = i // 2
            lam_init = 0.8 - 0.6 * math.exp(-0.3 * i)
            h = even_layer(h, positions, norm_g[i], w_in_even[e], b_forget[e], qn_a[e], kn_a[e],
                           qn_b[e], kn_b[e], lam_q1[e], lam_k1[e], lam_q2[e], lam_k2[e],
                           subln_g[e], w_out_even[e], lam_init)
        else:
            o = i // 2
            h = odd_layer(h, positions, norm_g[i], w_in_odd[o], qn_c[o], kn_c[o], w_out_odd[o])
        h = per_layer_embedding(h, p[i], w_ple[i], w_ple_gate[i])
    return h
```

```python
import math
import numpy as np
import concourse.bass as bass
import concourse.mybir as mybir
from concourse.bass_utils import run_bass_kernel_spmd

F32 = mybir.dt.float32
BF16 = mybir.dt.bfloat16
I32 = mybir.dt.int32
AF = mybir.ActivationFunctionType
ALU = mybir.AluOpType
AX = mybir.AxisListType

S = 4096
D = 1024
NT = 32
NG = 8
NPT = 6
KC = 8
ROPE_THETA = 500000.0
EPS = 1e-6
LAM_INIT0 = 0.8 - 0.6 * math.exp(0.0)

SM_G0, SM_G1, SM_GA, SM_GB, SM_GC, SM_LAM, SM_N = 0, 1024, 2048, 2304, 2560, 2816, 3072


class Res:
    __slots__ = ("name", "w", "rd", "excl")

    def __init__(self, name="", excl=False):
        self.name = name
        self.w = None
        self.rd = {}
        self.excl = excl


class Ins:
    __slots__ = ("eng", "fn", "deps", "isdma", "sem", "val", "needs_inc", "idx")


class Prog:
    ENGS = ("pe", "act", "dve", "pool", "sp")

    def __init__(self, nc, n_dma_sems=24):
        self.nc = nc
        self.streams = {e: [] for e in self.ENGS}
        self.n = 0
        self.last = {e: None for e in self.ENGS}
        self.open_dmas = []
        self.barrier_deps = {e: [] for e in self.ENGS}
        self.n_dma_sems = n_dma_sems

    def _add(self, eng, fn, reads, writes, isdma):
        ins = Ins()
        ins.eng, ins.fn, ins.isdma = eng, fn, isdma
        ins.sem = ins.val = None
        ins.needs_inc = False
        ins.idx = self.n
        self.n += 1
        deps = {}
        writes = list(writes) + [r for r in reads if r.excl]
        for r in reads:
            if r.w is not None:
                deps[r.w.idx] = r.w
        for w in writes:
            if w.w is not None:
                deps[w.w.idx] = w.w
            for i in w.rd.values():
                deps[i.idx] = i
        for d in self.barrier_deps[eng]:
            deps[d.idx] = d
        self.barrier_deps[eng] = []
        out = []
        for d in deps.values():
            if d is ins:
                continue
            if (not isdma) and (not d.isdma) and d.eng == "pe" and eng == "pe":
                continue
            d.needs_inc = True
            out.append(d)
        out.sort(key=lambda i: i.idx)
        ins.deps = out
        key = ("dma", ins.idx) if isdma else eng
        for r in reads:
            r.rd[key] = ins
        for w in writes:
            w.w = ins
            w.rd = {}
        self.streams[eng].append(ins)
        if isdma:
            self.open_dmas.append(ins)
        else:
            self.last[eng] = ins
        return ins

    def op(self, eng, fn, reads=(), writes=()):
        return self._add(eng, fn, reads, writes, False)

    def dma(self, fn, reads=(), writes=(), queue="sp"):
        return self._add(queue, fn, reads, writes, True)

    def barrier(self):
        deps = [i for i in self.last.values() if i is not None] + list(self.open_dmas)
        self.open_dmas = []
        for e in self.ENGS:
            self.barrier_deps[e] = list(deps)

    def emit(self):
        nc = self.nc
        sems = {e: nc.alloc_semaphore("sem_" + e) for e in ("pe", "act", "dve", "pool")}
        rings = {"sp": [nc.alloc_semaphore("sem_dma_sp%d" % i) for i in range(self.n_dma_sems)],
                 "pool": [nc.alloc_semaphore("sem_dma_pool%d" % i) for i in range(8)]}
        dcount = {q: [0] * len(r) for q, r in rings.items()}
        kq = {q: 0 for q in rings}
        cnt = {e: 0 for e in sems}
        dprev = {}
        allins = sorted((i for s in self.streams.values() for i in s), key=lambda i: i.idx)
        for ins in allins:
            if ins.isdma:
                q = ins.eng
                j = kq[q] % len(rings[q])
                kq[q] += 1
                prev = (rings[q][j], dcount[q][j]) if dcount[q][j] > 0 else None
                dprev[ins.idx] = prev
                dcount[q][j] += 16
                ins.sem, ins.val = rings[q][j], dcount[q][j]
            elif ins.needs_inc:
                cnt[ins.eng] += 1
                ins.sem, ins.val = sems[ins.eng], cnt[ins.eng]
        final_dma = [(rings[q][j], dcount[q][j]) for q in rings for j in range(len(rings[q])) if dcount[q][j] > 0]

        def run(eng_name, e):
            seen = {}

            def wait(sem, val):
                key = id(sem)
                if seen.get(key, 0) >= val:
                    return
                e.wait_ge(sem, val)
                seen[key] = val

            for ins in self.streams[eng_name]:
                for d in ins.deps:
                    wait(d.sem, d.val)
                if ins.isdma and dprev[ins.idx] is not None:
                    wait(*dprev[ins.idx])
                bi = ins.fn(e)
                if ins.isdma:
                    bi.then_inc(ins.sem, 16)
                elif ins.needs_inc:
                    bi.then_inc(ins.sem, 1)
            if eng_name == "sp":
                for sem, val in final_dma:
                    wait(sem, val)

        with nc.Block() as block:
            @block.tensor
            def _(e):
                run("pe", e)

            @block.scalar
            def _(e):
                run("act", e)

            @block.vector
            def _(e):
                run("dve", e)

            @block.gpsimd
            def _(e):
                run("pool", e)

            @block.sync
            def _(e):
                run("sp", e)


class Builder:
    def __init__(self, dbg=False, stop_after=None):
        self.dbg = dbg
        self.stop_after = stop_after
        nc = bass.Bass("TRN2", target_bir_lowering=False)
        self.nc = nc
        self.P = Prog(nc)
        ext = lambda name, shape, dt=F32: nc.dram_tensor(name, list(shape), dt, kind="ExternalInput").ap()
        self.x = ext("x", [S, D])
        self.p = ext("p", [2, S, 256])
        self.pos = ext("pos", [128, NT], I32)
        self.smalls = ext("smalls", [1, SM_N])
        self.cols = ext("cols", [128, 4])
        self.w_in0 = ext("w_in0", [D, 8 * 512])
        self.w_f = ext("w_f", [D, 128])
        self.w_in1 = ext("w_in1", [D, 8 * 512])
        self.w_out = ext("w_out", [2, D, D])
        self.w_ple = ext("w_ple", [2, 256, D])
        self.w_gate = ext("w_gate", [2, D, D])
        self.out = nc.dram_tensor("out", [S, D], F32, kind="ExternalOutput").ap()
        sk = "ExternalOutput" if dbg else "Internal"
        if dbg:
            self.MT = [nc.dram_tensor("mt%d" % l, [D, S], BF16, kind=sk).ap() for l in range(2)]
            self.H = nc.dram_tensor("hmid", [S, D], F32, kind=sk).ap()
            self.r_MT = [[[Res() for _ in range(NG)] for _ in range(8)] for _ in range(2)]
            self.r_H = [Res() for _ in range(NT)]
            self.r_out = [Res() for _ in range(NT)]
        else:
            mt = nc.dram_tensor("mt", [D, S], BF16, kind="Internal").ap()
            self.MT = [mt, mt]
            rm = [[Res() for _ in range(NG)] for _ in range(8)]
            self.r_MT = [rm, rm]
            self.H = self.out
            self.r_H = [Res() for _ in range(NT)]
            self.r_out = self.r_H
        self._stack = []

    def tap(self, name, ap, reads):
        if not self.dbg:
            return
        t = self.nc.dram_tensor("tap_" + name, list(ap.shape), ap.dtype, kind="ExternalOutput").ap()
        self.P.dma(lambda e: e.dma_start(out=t, in_=ap), reads=reads)

    def sb(self, name, shape, dt):
        return self.nc.alloc_sbuf_tensor(name, list(shape), dt)

    def build(self):
        nc, P = self.nc, self.P
        self.banks = [nc.alloc_psum_tensor("bank%d" % i, [128, 512], F32) for i in range(8)]
        self.r_bank = [Res("bank%d" % i, excl=True) for i in range(8)]
        self.setup_constants()
        for layer in range(2):
            self.layer_mixer(layer)
            if self.stop_after == ("mix", layer):
                break
            P.barrier()
            if layer == 0:
                self.strip_t = nc.alloc_sbuf_tensor("strip", [128, 2944], BF16)
                self.r_strip = Res()
            self.layer_tail(layer)
            if self.stop_after == ("tail", layer):
                break
            P.barrier()
        P.emit()
        return nc

    def setup_constants(self):
        nc, P = self.nc, self.P
        sb = self.sb
        self.iota_d = sb("iota_d", [128, 128], I32)
        self.ident_bf = sb("ident_bf", [128, 128], BF16)
        self.ident_f = sb("ident_f", [128, 128], F32)
        self.tri_bf = sb("tri_bf", [128, 128], BF16)
        self.ones_bf = sb("ones_bf", [128, 128], BF16)
        self.r_const = Res("const")
        rc = self.r_const
        P.op("pool", lambda e: e.iota(self.iota_d[:], [[1, 128]], base=0, channel_multiplier=-1), writes=[rc])
        P.op("dve", lambda e: e.tensor_scalar(self.ident_bf[:], self.iota_d[:], 0, None, ALU.is_equal), reads=[rc], writes=[rc])
        P.op("dve", lambda e: e.tensor_scalar(self.ident_f[:], self.iota_d[:], 0, None, ALU.is_equal), reads=[rc], writes=[rc])
        P.op("dve", lambda e: e.tensor_scalar(self.tri_bf[:], self.iota_d[:], 0, None, ALU.is_ge), reads=[rc], writes=[rc])
        P.op("dve", lambda e: e.memset(self.ones_bf[:], 1.0), writes=[rc])
        self.sm = sb("smalls_sb", [128, SM_N], F32)
        self.colsb = sb("cols_sb", [128, 4], F32)
        P.dma(lambda e: e.dma_start(out=self.sm[:], in_=self.smalls.partition_broadcast(128)), writes=[rc])
        P.dma(lambda e: e.dma_start(out=self.colsb[:], in_=self.cols), writes=[rc])
        P.op("dve", lambda e: e.tensor_scalar(self.sm[:, SM_GA:SM_GA + 128], self.sm[:, SM_GA:SM_GA + 128], 0.125, None, ALU.mult),
             reads=[rc], writes=[rc])
        sc = 128.0 ** -0.5
        P.op("dve", lambda e: e.tensor_scalar(self.sm[:, SM_GB:SM_GB + 128], self.sm[:, SM_GB:SM_GB + 128], sc, None, ALU.mult),
             reads=[rc], writes=[rc])
        P.op("dve", lambda e: e.tensor_scalar(self.sm[:, SM_GC:SM_GC + 128], self.sm[:, SM_GC:SM_GC + 128], sc, None, ALU.mult),
             reads=[rc], writes=[rc])
        self.lamw = sb("lamw", [128, 128], F32)
        self.lams = sb("lams", [128, 8], F32)
        L = SM_LAM
        P.op("dve", lambda e: e.tensor_tensor(self.lamw[:, 0:64], self.sm[:, L:L + 64], self.sm[:, L + 64:L + 128], ALU.mult), reads=[rc], writes=[rc])
        P.op("dve", lambda e: e.tensor_tensor(self.lamw[:, 64:128], self.sm[:, L + 128:L + 192], self.sm[:, L + 192:L + 256], ALU.mult), reads=[rc], writes=[rc])
        P.op("dve", lambda e: e.tensor_reduce(self.lams[:, 0:2], self.lamw[:].rearrange("p (a b) -> p a b", a=2), AX.X, ALU.add), reads=[rc], writes=[rc])
        P.op("act", lambda e: e.activation(self.lams[:, 2:4], self.lams[:, 0:2], AF.Exp), reads=[rc], writes=[rc])
        P.op("dve", lambda e: e.tensor_tensor(self.lams[:, 4:5], self.lams[:, 3:4], self.lams[:, 2:3], ALU.subtract), reads=[rc], writes=[rc])
        P.op("dve", lambda e: e.tensor_scalar(self.lams[:, 4:5], self.lams[:, 4:5], -LAM_INIT0, None, ALU.add), reads=[rc], writes=[rc])
        P.op("dve", lambda e: e.tensor_scalar(self.lams[:, 5:6], self.colsb[:, 0:1], 1.0 - LAM_INIT0, None, ALU.mult), reads=[rc], writes=[rc])
        self.posf = sb("posf", [128, NT], F32)
        self.posi = sb("posi", [128, NT], I32)
        P.dma(lambda e: e.dma_start(out=self.posi[:], in_=self.pos), writes=[rc])
        P.op("dve", lambda e: e.tensor_copy(self.posf[:], self.posi[:]), reads=[rc], writes=[rc])
        self.rope = {}
        import contextlib
        for half, rep in ((8, 4), (16, 2)):
            self.rope[half] = [sb("rcs%d_%d" % (half, i), [128, NT, rep, half], F32) for i in range(2)]
        es = contextlib.ExitStack()
        tsb = lambda name, shape, dt: es.enter_context(nc.sbuf_tensor(name, list(shape), dt))
        for half, rep in ((8, 4), (16, 2)):
            n = NT * half
            jj = tsb("rj%d" % half, [128, half], I32)
            fr = tsb("rf%d" % half, [128, half], F32)
            ang = tsb("ra%d" % half, [128, NT, half], F32)
            tmp = tsb("rt%d" % half, [128, NT, half], F32)
            tmi = tsb("ri%d" % half, [128, NT, half], I32)
            cs = self.rope[half]
            P.op("pool", lambda e, jj=jj, half=half: e.iota(jj[:], [[1, half]], base=0, channel_multiplier=0), writes=[rc])
            P.op("dve", lambda e, jj=jj, fr=fr: e.tensor_copy(fr[:], jj[:]), reads=[rc], writes=[rc])
            P.op("act", lambda e, fr=fr, half=half: e.activation(fr[:], fr[:], AF.Exp, bias=-math.log(2 * math.pi), scale=-math.log(ROPE_THETA) / half),
                 reads=[rc], writes=[rc])
            for t in range(NT):
                P.op("dve", lambda e, t=t, ang=ang, fr=fr: e.tensor_scalar(ang[:, t, :], fr[:], self.posf[:, t:t + 1], None, ALU.mult), reads=[rc], writes=[rc])
            for i, shift in enumerate((0.25, 0.0)):
                flat = lambda a: a[:].rearrange("p t h -> p (t h)")
                P.op("dve", lambda e, shift=shift, tmp=tmp, ang=ang: e.tensor_scalar(flat(tmp), flat(ang), shift, None, ALU.add), reads=[rc], writes=[rc])
                P.op("dve", lambda e, tmp=tmp, tmi=tmi: e.tensor_copy(flat(tmi), flat(tmp)), reads=[rc], writes=[rc])
                P.op("dve", lambda e, tmp=tmp, tmi=tmi: e.tensor_tensor(flat(tmp), flat(tmp), flat(tmi), ALU.subtract),
                     reads=[rc], writes=[rc])
                g1 = tsb("rg%d_%d" % (half, i), [128, n], F32)
                P.op("dve", lambda e, g1=g1, tmp=tmp: e.tensor_scalar(g1[:], flat(tmp), 0.5, None, ALU.is_gt), reads=[rc], writes=[rc])
                P.op("dve", lambda e, g1=g1, tmp=tmp: e.tensor_tensor(flat(tmp), flat(tmp), g1[:], ALU.subtract), reads=[rc], writes=[rc])
                P.op("dve", lambda e, g1=g1, tmp=tmp: e.tensor_scalar(g1[:], flat(tmp), -0.5, None, ALU.is_lt), reads=[rc], writes=[rc])
                P.op("dve", lambda e, g1=g1, tmp=tmp: e.tensor_tensor(flat(tmp), flat(tmp), g1[:], ALU.add), reads=[rc], writes=[rc])
                for r in range(rep):
                    P.op("act", lambda e, r=r, i=i, cs=cs, tmp=tmp: e.activation(cs[i][:, :, r, :], tmp[:], AF.Sin, scale=6.28318),
                         reads=[rc], writes=[rc])
        es.close()
        P.barrier()

    def layer_mixer(self, layer):
        import contextlib
        nc, P = self.nc, self.P
        B, RB = self.banks, self.r_bank
        rc = self.r_const
        src = self.x if layer == 0 else self.H
        w_in = self.w_in0 if layer == 0 else self.w_in1
        g_off = SM_G0 if layer == 0 else SM_G1
        with contextlib.ExitStack() as es:
            def sb(name, shape, dt):
                return es.enter_context(nc.sbuf_tensor("L%d_%s" % (layer, name), list(shape), dt))
            hnT = sb("hnT", [128, KC, S], BF16)
            r_hnT = [Res() for _ in range(NT)]
            mhalf = sb("mhalf", [128, 32], F32)
            es1 = contextlib.ExitStack()
            tsb = lambda name, shape, dt: es1.enter_context(nc.sbuf_tensor("L%d_%s" % (layer, name), list(shape), dt))
            NR = 4
            xts = [tsb("xt%d" % i, [128, D], F32) for i in range(NR)]
            r_xt = [Res() for _ in range(NR)]
            junks = [tsb("junk%d" % i, [128, D], F32) for i in range(2)]
            r_junks = [Res() for _ in range(2)]
            nstat = tsb("nstat", [128, 4 * NT], F32)
            r_nstat = [Res() for _ in range(NT)]
            hnb = [tsb("hnb%d" % i, [128, D], BF16) for i in range(2)]
            r_hnb = [Res() for _ in range(2)]
            P.op("dve", lambda e: e.memset(mhalf[:], -0.5), writes=[rc])

            def n_load(t):
                xt, rx = xts[t % NR], r_xt[t % NR]
                P.dma(lambda e, xt=xt, t=t: e.dma_start(out=xt[:], in_=src[t * 128:(t + 1) * 128, :]),
                      reads=([self.r_H[t]] if layer == 1 else []), writes=[rx])

            def n_stats(t):
                xt, rx = xts[t % NR], r_xt[t % NR]
                junk, r_junk = junks[t % 2], r_junks[t % 2]
                P.op("act", lambda e, xt=xt, junk=junk: e.activation(junk[:], xt[:], AF.Square), reads=[rx], writes=[r_junk])
                P.op("dve", lambda e, t=t, junk=junk: e.tensor_reduce(nstat[:, t:t + 1], junk[:], AX.X, ALU.add), reads=[r_junk], writes=[r_nstat[t]])
                P.op("dve", lambda e, t=t: e.tensor_scalar(nstat[:, NT + t:NT + t + 1], nstat[:, t:t + 1], 1.0 / D, EPS, ALU.mult, ALU.add),
                     reads=[r_nstat[t]], writes=[r_nstat[t]])
                P.op("pool", lambda e, t=t: e.tensor_tensor(nstat[:, 2 * NT + t:2 * NT + t + 1], nstat[:, NT + t:NT + t + 1], mhalf[:, 0:1], ALU.pow),
                     reads=[r_nstat[t], rc], writes=[r_nstat[t]])

            def n_apply(t):
                xt, rx = xts[t % NR], r_xt[t % NR]
                hb, rh = hnb[t % 2], r_hnb[t % 2]
                P.op("dve", lambda e, xt=xt, hb=hb, t=t: e.scalar_tensor_tensor(hb[:], xt[:], nstat[:, 2 * NT + t:2 * NT + t + 1],
                                                                                 self.sm[:, g_off:g_off + D], ALU.mult, ALU.mult),
                     reads=[rx, r_nstat[t], rc], writes=[rh])
                bk = 2 + (t % 2)
                bkb = B[bk][:].bitcast(BF16)
                for kc in range(KC):
                    P.op("pe", lambda e, hb=hb, kc=kc, bkb=bkb: e.transpose(bkb[:, kc * 128:(kc + 1) * 128], hb[:, kc * 128:(kc + 1) * 128], self.ident_bf[:]),
                         reads=[rh, rc], writes=[RB[bk]])
                P.op("dve", lambda e, t=t, bkb=bkb: e.tensor_copy(hnT[:, :, t * 128:(t + 1) * 128], bkb.rearrange("p (c n) -> p c n", c=KC)),
                     reads=[RB[bk]], writes=[r_hnT[t]])

            for t in range(min(3, NT)):
                n_load(t)
            n_stats(0)
            n_stats(1)
            for t in range(NT):
                if t + 3 < NT:
                    n_load(t + 3)
                if t + 2 < NT:
                    n_stats(t + 2)
                n_apply(t)
            self.tap("hnT%d" % layer, hnT[:], r_hnT)
            self.tap("nstat%d" % layer, nstat[:], r_nstat)
            if layer == 0:
                self.tap("cos8", self.rope[8][0][:], [rc])
                self.tap("sin8", self.rope[8][1][:], [rc])
                self.tap("cos16", self.rope[16][0][:], [rc])
                self.tap("sin16", self.rope[16][1][:], [rc])
                self.tap("lams", self.lams[:], [rc])
                self.tap("sm", self.sm[:], [rc])
            es1.close()
            P.barrier()
            fox = None
            strip = None
            es2 = contextlib.ExitStack()
            tsb2 = lambda name, shape, dt: es2.enter_context(nc.sbuf_tensor("L%d_%s" % (layer, name), list(shape), dt))
            if layer == 0:
                fox = self.fox_prepare(sb, tsb2, hnT, r_hnT)
            else:
                strip = dict(strip=self.strip_t, r_strip=self.r_strip)
            if layer == 0:
                self.tap("CR", fox["CR"][:], fox["r_CR"])
                self.tap("negc", fox["negc"][:], fox["r_negc"])
                self.tap("sel", fox["sel"][:], [rc])
            else:
                self.tap("strip", strip["strip"][:], [strip["r_strip"]])
            es2.close()
            P.barrier()
            QT = sb("QT", [128, S], BF16)
            KT = sb("KT", [128, S], BF16)
            V = sb("V", [128, NT, 128], BF16)
            ZT = sb("ZT", [128, S], BF16)
            r_QT = [Res() for _ in range(NG)]
            r_KT = [Res() for _ in range(NG)]
            r_V = [Res() for _ in range(NT)]
            r_ZT = [Res() for _ in range(NG)]
            Wt = [sb("W%d" % i, [128, KC, 512], BF16) for i in range(2)]
            r_W = [[Res(), Res()] for _ in range(2)]
            sq = [sb("sq%d" % i, [128, 4, 256], F32) for i in range(2)]
            r_sq = [[Res() for _ in range(4)] for _ in range(2)]
            qkf = [sb("qkf%d" % i, [128, 4, 256], F32) for i in range(2)]
            r_qkf = [[Res() for _ in range(4)] for _ in range(2)]
            pst = sb("pst", [128, 48], F32)
            r_pst = Res()
            qn = sb("qn", [128, 4, 256], F32)
            r_qn = [Res() for _ in range(16)]
            qkb = [sb("qkb%d" % i, [128, 4, 256], BF16) for i in range(2)]
            r_qkb = [Res() for _ in range(2)]
            rt = sb("rt", [128, 4, 128], F32)
            r_rt = [Res() for _ in range(4)]
            sig = sb("sig", [128, 512], F32)
            r_sig = Res()
            A = {}
            A["pt"] = [sb("pt%d" % i, [128, 512], BF16) for i in range(NPT)]
            A["r_pt"] = [Res() for _ in range(NPT)]
            for nm in ("rec", "o1", "dd", "rs", "tz"):
                A[nm] = sb(nm, [128, 512], F32)
                A["r_" + nm] = Res()
            A["sqb"] = sb("sqb", [128, 512], BF16)
            A["r_sqb"] = Res()
            A["mts"] = [sb("mts%d" % i, [128, 512], BF16) for i in range(2)]
            A["r_mts"] = [Res() for _ in range(2)]
            A["cnt"] = 0
            A["ptc"] = 0
            A["mtc"] = 0
            A["print_free"] = nc.sbuf_bytes_remaining
            if layer == 0:
                A["qz"] = [[sb("qz%d_%d" % (m, i), [128, 512], BF16) for i in range(2)] for m in range(2)]
                A["r_qz"] = [[Res() for _ in range(2)] for _ in range(2)]
                A["qzc"] = 0
                for m in range(2):
                    for i in range(2):
                        P.op("pool", lambda e, m=m, i=i: e.memset(A["qz"][m][i][:], 0.0), writes=[A["r_qz"][m][i]])

            def load_w(u):
                W, rw = Wt[u % 2], r_W[u % 2]
                for half in range(2):
                    P.dma(lambda e, W=W, u=u, half=half: e.dma_start(
                        out=W[:, half * 4:(half + 1) * 4, :],
                        in_=w_in[half * 512:(half + 1) * 512, u * 512:(u + 1) * 512].rearrange("(c p) n -> p c n", p=128)),
                        writes=[rw[half]], queue="pool")

            load_w(0)
            pj_i = 0
            for u in range(8):
                if u + 1 < 8:
                    load_w(u + 1)
                W, rw = Wt[u % 2], r_W[u % 2]
                if layer == 0 and u < 4:
                    kind, G, F, gain_off, half = "diff", 4, 64, SM_GA, 8
                elif layer == 0:
                    kind, G, F, gain_off, half = "fox", 2, 128, SM_GB, 0
                else:
                    kind, G, F, gain_off, half = "dil", 2, 128, SM_GC, 16
                trb = B[3][:].bitcast(BF16)

                def stage_qkv(tg, W=W, rw=rw):
                    nonlocal pj_i
                    s = tg % 2
                    for t4 in range(4):
                        t = tg * 4 + t4
                        bk = pj_i % 2
                        pj_i += 1
                        pj = B[bk]
                        for kc in range(KC):
                            P.op("pe", lambda e, pj=pj, kc=kc, t=t, W=W: e.matmul(pj[:, 0:384], hnT[:, kc, t * 128:(t + 1) * 128], W[:, kc, 0:384],
                                                                             start=(kc == 0), stop=(kc == KC - 1)),
                                 reads=[r_hnT[t], rw[kc // 4]], writes=[RB[bk]])
                        P.op("act", lambda e, pj=pj, t4=t4, s=s: e.activation(sq[s][:, t4, :], pj[:, 0:256], AF.Square), reads=[RB[bk]], writes=[r_sq[s][t4]])
                        P.op("act", lambda e, pj=pj, t4=t4, s=s: e.copy(qkf[s][:, t4, :], pj[:, 0:256]), reads=[RB[bk]], writes=[r_qkf[s][t4]])
                        P.op("act", lambda e, pj=pj, t=t: e.copy(V[:, t, :], pj[:, 256:384]), reads=[RB[bk]], writes=[r_V[t]])

                def stage_z(tg, W=W, rw=rw):
                    for kc in range(KC):
                        P.op("pe", lambda e, kc=kc, tg=tg, W=W: e.matmul(B[2][:, :], W[:, kc, 384:512], hnT[:, kc, tg * 512:(tg + 1) * 512],
                                                                    start=(kc == 0), stop=(kc == KC - 1)),
                             reads=r_hnT[tg * 4:tg * 4 + 4] + [rw[kc // 4]], writes=[RB[2]])
                    P.op("act", lambda e: e.activation(sig[:], B[2][:], AF.Sigmoid), reads=[RB[2]], writes=[r_sig])
                    P.op("dve", lambda e, tg=tg: e.tensor_tensor(ZT[:, tg * 512:(tg + 1) * 512], B[2][:], sig[:], ALU.mult), reads=[RB[2], r_sig], writes=[r_ZT[tg]])

                def stage_norm(tg, G=G, F=F, gain_off=gain_off, half=half):
                    s = tg % 2
                    P.op("dve", lambda e, s=s: e.tensor_reduce(pst[:, 0:4 * G], sq[s][:].rearrange("p t (g f) -> p (t g) f", f=F), AX.X, ALU.add),
                         reads=r_sq[s], writes=[r_pst])
                    P.op("dve", lambda e: e.tensor_scalar(pst[:, 16:16 + 4 * G], pst[:, 0:4 * G], 1.0 / F, EPS, ALU.mult, ALU.add), reads=[r_pst], writes=[r_pst])
                    P.op("pool", lambda e: e.tensor_tensor(pst[:, 32:32 + 4 * G], pst[:, 16:16 + 4 * G], mhalf[:, 0:4 * G], ALU.pow), reads=[r_pst, rc], writes=[r_pst])
                    for t4 in range(4):
                        for g in range(G):
                            P.op("dve", lambda e, t4=t4, g=g, s=s: e.scalar_tensor_tensor(
                                qn[:, t4, g * F:(g + 1) * F], qkf[s][:, t4, g * F:(g + 1) * F], pst[:, 32 + t4 * G + g:32 + t4 * G + g + 1],
                                self.sm[:, gain_off + g * F:gain_off + (g + 1) * F], ALU.mult, ALU.mult),
                                reads=[r_qkf[s][t4], r_pst, rc], writes=[r_qn[t4 * G + g]])
                    rqn = r_qn[0:4 * G]
                    if half:
                        cs = self.rope[half]
                        q4 = qn[:].rearrange("p t (g f) -> p (t g) f", f=F)
                        x1, x2 = q4[:, :, 0:half], q4[:, :, half:2 * half]
                        cos = cs[0][:, tg * 4:(tg + 1) * 4, :, :].rearrange("p t g h -> p (t g) h")
                        sin = cs[1][:, tg * 4:(tg + 1) * 4, :, :].rearrange("p t g h -> p (t g) h")
                        rv = lambda i: rt[:, i, 0:4 * G * half].rearrange("p (g h) -> p g h", h=half)
                        for i, (a, b_) in enumerate(((x1, cos), (x2, sin), (x2, cos), (x1, sin))):
                            P.op("dve", lambda e, i=i, a=a, b_=b_, rv=rv: e.tensor_tensor(rv(i), a, b_, ALU.mult), reads=rqn + [rc], writes=[r_rt[i]])

                def stage_norm_b(tg, G=G, F=F, half=half):
                    s = tg % 2
                    rqn = r_qn[0:4 * G]
                    qb_, rqb = qkb[s], r_qkb[s]
                    P.op("act", lambda e, qb_=qb_: e.copy(qb_[:], qn[:]), reads=rqn, writes=[rqb])
                    if half:
                        rv = lambda i: rt[:, i, 0:4 * G * half].rearrange("p (g h) -> p g h", h=half)
                        qb4 = qb_[:].rearrange("p t (g f) -> p (t g) f", f=F)
                        P.op("dve", lambda e, qb4=qb4, rv=rv: e.tensor_tensor(qb4[:, :, 0:half], rv(0), rv(1), ALU.subtract), reads=r_rt[0:2], writes=[rqb])
                        P.op("dve", lambda e, qb4=qb4, rv=rv: e.tensor_tensor(qb4[:, :, half:2 * half], rv(2), rv(3), ALU.add), reads=r_rt[2:4], writes=[rqb])

                def stage_tr(tg):
                    s = tg % 2
                    qb_, rqb = qkb[s], r_qkb[s]
                    for t4 in range(4):
                        P.op("pe", lambda e, qb_=qb_, t4=t4: e.transpose(trb[:, t4 * 128:(t4 + 1) * 128], qb_[:, t4, 0:128], self.ident_bf[:]),
                             reads=[rqb, rc], writes=[RB[3]])
                        P.op("pe", lambda e, qb_=qb_, t4=t4: e.transpose(trb[:, 512 + t4 * 128:512 + (t4 + 1) * 128], qb_[:, t4, 128:256], self.ident_bf[:]),
                             reads=[rqb, rc], writes=[RB[3]])
                    P.op("act", lambda e, tg=tg: e.copy(QT[:, tg * 512:(tg + 1) * 512], trb[:, 0:512]), reads=[RB[3]], writes=[r_QT[tg]])
                    P.op("dve", lambda e, tg=tg: e.tensor_copy(KT[:, tg * 512:(tg + 1) * 512], trb[:, 512:1024]), reads=[RB[3]], writes=[r_KT[tg]])

                att = lambda qts, u=u, kind=kind: self.attention_unit(layer, u, kind, QT, KT, V, ZT, r_QT, r_KT, r_V, r_ZT, A, fox, strip, qts=qts)
                stage_qkv(0)
                stage_z(0)
                for tg in range(NG):
                    if tg + 1 < NG:
                        stage_qkv(tg + 1)
                    stage_norm(tg)
                    if tg + 1 < NG:
                        stage_z(tg + 1)
                    if tg >= 1:
                        stage_tr(tg - 1)
                    stage_norm_b(tg)
                att(range(0, 3))
                stage_tr(NG - 1)
                if u in (0, 4):
                    self.tap("QT_%d_%d" % (layer, u), QT[:], r_QT)
                    self.tap("KT_%d_%d" % (layer, u), KT[:], r_KT)
                    self.tap("V_%d_%d" % (layer, u), V[:], r_V)
                    self.tap("ZT_%d_%d" % (layer, u), ZT[:], r_ZT)
                att(range(3, NG))

    def fox_prepare(self, sb, tsb, hnT, r_hnT):
        nc, P = self.nc, self.P
        B, RB = self.banks, self.r_bank
        rc = self.r_const
        wf = sb("wf", [128, KC, 128], BF16)
        r_wf = Res()
        P.dma(lambda e: e.dma_start(out=wf[:], in_=self.w_f.rearrange("(c p) n -> p c n", p=128)), writes=[r_wf], queue="pool")
        CR = sb("CR", [128, S], BF16)
        r_CR = [Res() for _ in range(NG)]
        negc = sb("negc", [128, NT, 4], F32)
        r_negc = [Res() for _ in range(NG)]
        negb = sb("negb", [128, 1], F32)
        P.op("dve", lambda e: e.tensor_scalar(negb[:], self.colsb[:, 1:2], -1.0, None, ALU.mult), reads=[rc], writes=[rc])
        sel = sb("sel", [128, 4, 128], BF16)
        pidx = sb("pidx", [128, 128], I32)
        P.op("pool", lambda e: e.iota(pidx[:], [[0, 128]], base=0, channel_multiplier=1), writes=[rc])
        P.op("dve", lambda e: e.tensor_scalar(pidx[:], pidx[:], 31, None, ALU.bitwise_and), reads=[rc], writes=[rc])
        for h in range(4):
            P.op("dve", lambda e, h=h: e.tensor_scalar(sel[:, h, :], pidx[:], h, None, ALU.is_equal), reads=[rc], writes=[rc])
        P.op("dve", lambda e: e.memset(CR[:], 0.0), writes=r_CR)
        ones_f = tsb("ones_f", [128, 512], F32)
        P.op("dve", lambda e: e.memset(ones_f[:], 1.0), writes=[rc])
        E = tsb("fE", [128, 512], F32)
        Lg = tsb("fL", [128, 512], F32)
        C = [tsb("fC%d" % i, [128, 512], F32) for i in range(2)]
        r1 = tsb("fr1", [128, 512], F32)
        r2 = tsb("fr2", [128, 512], F32)
        hb = tsb("fhb", [128, 512], BF16)
        mb = tsb("fmb", [128, 512], BF16)
        rw = Res()
        r_E, r_L = Res(), Res()
        r_C = [Res() for _ in range(2)]
        for tg in range(NG):
            cs = slice(tg * 512, (tg + 1) * 512)
            for kc in range(KC):
                P.op("pe", lambda e, kc=kc, cs=cs: e.matmul(B[4][:, :], wf[:, kc, :], hnT[:, kc, cs], start=(kc == 0), stop=(kc == KC - 1)),
                     reads=r_hnT[tg * 4:tg * 4 + 4] + [r_wf], writes=[RB[4]])
            P.op("act", lambda e: e.activation(E[:], B[4][:], AF.Exp, bias=negb[:, 0:1], scale=-1.0), reads=[RB[4], rc], writes=[r_E])
            P.op("act", lambda e: e.activation(Lg[:], E[:], AF.Ln, bias=1.0), reads=[r_E], writes=[r_L])
            Cc, Cp = C[tg % 2], C[(tg + 1) % 2]
            init = 0.0 if tg == 0 else Cp[:, 511:512]
            P.op("dve", lambda e, Cc=Cc, init=init: e.tensor_tensor_scan(Cc[:], ones_f[:], Lg[:], init, ALU.mult, ALU.subtract),
                 reads=[r_L, rc, r_C[(tg + 1) % 2]], writes=[r_C[tg % 2]])
            rcc = r_C[tg % 2]
            P.op("dve", lambda e, Cc=Cc: e.tensor_copy(hb[:], Cc[:]), reads=[rcc], writes=[rw])
            P.op("dve", lambda e, Cc=Cc: e.tensor_tensor(r1[:], Cc[:], hb[:], ALU.subtract), reads=[rcc, rw], writes=[rw])
            P.op("dve", lambda e: e.tensor_copy(mb[:], r1[:]), reads=[rw], writes=[rw])
            P.op("dve", lambda e: e.tensor_tensor(r2[:], r1[:], mb[:], ALU.subtract), reads=[rw], writes=[rw])
            P.op("dve", lambda e, cs=cs: e.tensor_copy(CR[0:32, cs], hb[0:32, :]), reads=[rw], writes=[r_CR[tg]])
            P.op("dve", lambda e, cs=cs: e.tensor_copy(CR[32:64, cs], mb[32:64, :]), reads=[rw], writes=[r_CR[tg]])
            P.op("dve", lambda e, cs=cs: e.tensor_copy(CR[64:96, cs], r2[64:96, :]), reads=[rw], writes=[r_CR[tg]])
            for t4 in range(4):
                P.op("pe", lambda e, Cc=Cc, t4=t4: e.transpose(B[5][:, t4 * 32:(t4 + 1) * 32], Cc[0:32, t4 * 128:(t4 + 1) * 128], self.ident_f[0:32, 0:32]),
                     reads=[rcc, rc], writes=[RB[5]])
            P.op("dve", lambda e, tg=tg: e.tensor_scalar(negc[:, tg * 4:(tg + 1) * 4, :], B[5][:, 0:128].rearrange("p (t r) -> p t r", r=32)[:, :, 0:4],
                                                         -1.0, None, ALU.mult), reads=[RB[5]], writes=[r_negc[tg]])
        return dict(CR=CR, r_CR=r_CR, negc=negc, r_negc=r_negc, sel=sel)

    def mask_strip(self, sb, tsb):
        nc, P = self.nc, self.P
        rc = self.r_const
        WID = 2944
        strip = self.strip_t
        r_strip = self.r_strip
        CW = 736
        dI = tsb("mdI", [128, CW], I32)
        tI = tsb("mtI", [128, CW], I32)
        Am = tsb("mA", [128, CW], BF16)
        m1 = tsb("mm1", [128, CW], BF16)
        ee = tsb("mee", [128, CW], BF16)
        m2 = tsb("mm2", [128, CW], BF16)
        m3 = tsb("mm3", [128, CW], BF16)
        rw = Res()
        for c in range(WID // CW):
            c0 = c * CW
            P.op("pool", lambda e, c0=c0: e.iota(dI[:], [[1, CW]], base=c0 - 384, channel_multiplier=-1), writes=[rw])
            P.op("dve", lambda e: e.tensor_scalar(Am[:], dI[:], 0, None, ALU.is_ge), reads=[rw], writes=[rw])
            P.op("dve", lambda e: e.scalar_tensor_tensor(m1[:], dI[:], 128.0, Am[:], ALU.is_le, ALU.mult), reads=[rw], writes=[rw])
            P.op("dve", lambda e: e.tensor_scalar(tI[:], dI[:], 3, None, ALU.bitwise_and), reads=[rw], writes=[rw])
            P.op("dve", lambda e: e.tensor_scalar(ee[:], tI[:], 0, None, ALU.is_equal), reads=[rw], writes=[rw])
            P.op("dve", lambda e: e.scalar_tensor_tensor(m2[:], dI[:], 512.0, ee[:], ALU.is_le, ALU.mult), reads=[rw], writes=[rw])
            P.op("dve", lambda e: e.tensor_scalar(tI[:], dI[:], 15, None, ALU.bitwise_and), reads=[rw], writes=[rw])
            P.op("dve", lambda e: e.tensor_scalar(ee[:], tI[:], 0, None, ALU.is_equal), reads=[rw], writes=[rw])
            P.op("dve", lambda e: e.scalar_tensor_tensor(m3[:], dI[:], 2048.0, ee[:], ALU.is_le, ALU.mult), reads=[rw], writes=[rw])
            P.op("dve", lambda e: e.tensor_tensor(m2[:], m2[:], m3[:], ALU.add), reads=[rw], writes=[rw])
            P.op("dve", lambda e: e.tensor_tensor(m2[:], m2[:], Am[:], ALU.mult), reads=[rw], writes=[rw])
            P.op("dve", lambda e, c0=c0: e.tensor_tensor(strip[:, c0:c0 + CW], m2[:], m1[:], ALU.add), reads=[rw], writes=[r_strip])
        return dict(strip=strip, r_strip=r_strip)

    def attention_unit(self, layer, u, kind, QT, KT, V, ZT, r_QT, r_KT, r_V, r_ZT, A, fox, strip, qts=None):
        nc, P = self.nc, self.P
        B, RB = self.banks, self.r_bank
        rc = self.r_const
        h = u % 4
        maps = [(0, 64), (64, 128)] if kind == "diff" else [(0, 128)]
        STB = (4, 5, 2, 3)
        LA = 3
        A["pending"] = None
        for qt in (range(NG) if qts is None else qts):
            q0 = qt * 512
            kb_lo = 0 if layer == 0 else max(0, 4 * qt - 16)
            kbs = list(range(kb_lo, 4 * qt + 4))
            if kind == "diff":
                qi = A["qzc"] % 2
                A["qzc"] += 1
                qzs = [A["qz"][m][qi] for m in range(2)]
                r_qzs = [A["r_qz"][m][qi] for m in range(2)]
                for m, (a0, a1) in enumerate(maps):
                    P.op("pool", lambda e, m=m, a0=a0, a1=a1, q0=q0, qzs=qzs: e.tensor_copy(qzs[m][a0:a1, :], QT[a0:a1, q0:q0 + 512]),
                         reads=[r_QT[qt]], writes=[r_qzs[m]])
            for mi, (p0, p1) in enumerate(maps):
                pair = A["cnt"] % 2
                A["cnt"] += 1
                ob, lb = (6, 7) if pair == 0 else (0, 1)
                OT, LS = B[ob], B[lb]
                n = len(kbs)

                def emit_qk(j):
                    kb = kbs[j]
                    off = max(0, kb * 128 - q0)
                    sbk = STB[j % 4]
                    st = B[sbk]
                    qg = qt
                    if kind == "diff":
                        qsrc, rqs = qzs[mi], r_qzs[mi]
                        P.op("pe", lambda e, st=st, kb=kb, off=off, qsrc=qsrc: e.matmul(st[:, off:512], KT[:, kb * 128:(kb + 1) * 128], qsrc[:, off:512],
                                                                                    start=True, stop=True),
                             reads=[r_KT[kb // 4], rqs], writes=[RB[sbk]])
                    else:
                        rds = [r_KT[kb // 4], r_QT[qg]]
                        P.op("pe", lambda e, st=st, kb=kb, off=off, p0=p0, p1=p1, q0=q0: e.matmul(st[:, off:512], KT[p0:p1, kb * 128:(kb + 1) * 128], QT[p0:p1, q0 + off:q0 + 512],
                                                                         start=True, stop=(kind != "fox")),
                             reads=rds, writes=[RB[sbk]])
                    if kind == "fox":
                        P.op("pe", lambda e, st=st, off=off, q0=q0: e.matmul(st[:, off:512], fox["sel"][:, h, :], fox["CR"][:, q0 + off:q0 + 512],
                                                                  start=False, stop=True),
                             reads=[fox["r_CR"][qt], rc], writes=[RB[sbk]])

                for j0 in range(min(LA, n)):
                    emit_qk(j0)
                for j in range(n):
                    if j + LA < n:
                        emit_qk(j + LA)
                    kb = kbs[j]
                    off = max(0, kb * 128 - q0)
                    sbk = STB[j % 4]
                    st = B[sbk]
                    pi = A["ptc"] % NPT
                    A["ptc"] += 1
                    pt, rp = A["pt"][pi], A["r_pt"][pi]
                    if kind == "fox":
                        P.op("act", lambda e, pt=pt, st=st, off=off, kb=kb: e.activation(pt[:, off:512], st[:, off:512], AF.Exp, bias=fox["negc"][:, kb, h:h + 1]),
                             reads=[RB[sbk], fox["r_negc"][kb // 4]], writes=[rp])
                    else:
                        P.op("act", lambda e, pt=pt, st=st, off=off: e.activation(pt[:, off:512], st[:, off:512], AF.Exp), reads=[RB[sbk]], writes=[rp])
                    if layer == 0:
                        if kb * 128 >= q0:
                            P.op("dve", lambda e, pt=pt, off=off: e.tensor_tensor(pt[:, off:off + 128], pt[:, off:off + 128], self.tri_bf[:], ALU.mult),
                                 reads=[rp, rc], writes=[rp])
                    else:
                        rel = kb - 4 * qt
                        c0 = (3 - rel) * 128
                        P.op("dve", lambda e, pt=pt, off=off, c0=c0: e.tensor_tensor(pt[:, off:512], pt[:, off:512], strip["strip"][:, c0 + off:c0 + 512], ALU.mult),
                             reads=[rp, strip["r_strip"]], writes=[rp])
                    P.op("pe", lambda e, OT=OT, pt=pt, off=off, kb=kb, j=j, n=n: e.matmul(OT[:, off:512], V[:, kb, :], pt[:, off:512], start=(j == 0), stop=(j == n - 1)),
                         reads=[rp, r_V[kb]], writes=[RB[ob]])
                    P.op("pe", lambda e, LS=LS, pt=pt, off=off, j=j, n=n: e.matmul(LS[:, off:512], self.ones_bf[:], pt[:, off:512], start=(j == 0), stop=(j == n - 1)),
                         reads=[rp, rc], writes=[RB[lb]])
                    if mi == 0 and j == min(2, n - 1) and A.get("pending") is not None:
                        A["pending"]()
                        A["pending"] = None
                P.op("act", lambda e, LS=LS: e.activation(A["rec"][:], LS[:], AF.Ln), reads=[RB[lb]], writes=[A["r_rec"]])
                P.op("act", lambda e: e.activation(A["rec"][:], A["rec"][:], AF.Exp, scale=-1.0), reads=[A["r_rec"]], writes=[A["r_rec"]])
                if kind == "diff":
                    dst, rdst = (A["o1"], A["r_o1"]) if mi == 0 else (A["dd"], A["r_dd"])
                    P.op("dve", lambda e, OT=OT, dst=dst: e.tensor_tensor(dst[:], OT[:], A["rec"][:], ALU.mult), reads=[RB[ob], A["r_rec"]], writes=[rdst])
            A["pending"] = self._make_epilogue(layer, u, kind, qt, q0, OT, ob, ZT, r_ZT, A)
        A["pending"]()
        A["pending"] = None

    def _make_epilogue(self, layer, u, kind, qt, q0, OT, ob, ZT, r_ZT, A):
        P = self.P
        B, RB = self.banks, self.r_bank
        rc = self.r_const

        def epi():
            mi_ = A["mtc"] % 2
            A["mtc"] += 1
            mts, rm = A["mts"][mi_], A["r_mts"][mi_]
            zs = ZT[:, q0:q0 + 512]
            self._epilogue_body(layer, u, kind, qt, q0, OT, ob, zs, r_ZT, A, mts, rm)
        return epi

    def _epilogue_body(self, layer, u, kind, qt, q0, OT, ob, zs, r_ZT, A, mts, rm):
        P = self.P
        B, RB = self.banks, self.r_bank
        rc = self.r_const
        if True:
            if kind == "diff":
                P.op("dve", lambda e: e.scalar_tensor_tensor(A["dd"][:], A["dd"][:], self.lams[:, 4:5], A["o1"][:], ALU.mult, ALU.add),
                     reads=[A["r_o1"], A["r_dd"], rc], writes=[A["r_dd"]])
                P.op("act", lambda e: e.activation(A["sqb"][:], A["dd"][:], AF.Square), reads=[A["r_dd"]], writes=[A["r_sqb"]])
                P.op("pe", lambda e: e.matmul(B[2][:, :], self.ones_bf[:], A["sqb"][:], start=True, stop=True), reads=[A["r_sqb"], rc], writes=[RB[2]])
                P.op("act", lambda e: e.activation(A["rs"][:], B[2][:], AF.Ln, bias=EPS, scale=1.0 / 128), reads=[RB[2]], writes=[A["r_rs"]])
                P.op("act", lambda e: e.activation(A["rs"][:], A["rs"][:], AF.Exp, scale=-0.5), reads=[A["r_rs"]], writes=[A["r_rs"]])
                P.op("dve", lambda e, zs=zs: e.tensor_tensor(A["tz"][:], A["rs"][:], zs, ALU.mult), reads=[A["r_rs"], r_ZT[qt]], writes=[A["r_tz"]])
                P.op("dve", lambda e, mts=mts: e.scalar_tensor_tensor(mts[:], A["dd"][:], self.lams[:, 5:6], A["tz"][:], ALU.mult, ALU.mult),
                     reads=[A["r_dd"], A["r_tz"], rc], writes=[rm])
            else:
                P.op("dve", lambda e, zs=zs: e.tensor_tensor(A["tz"][:], A["rec"][:], zs, ALU.mult), reads=[A["r_rec"], r_ZT[qt]], writes=[A["r_tz"]])
                P.op("dve", lambda e, OT=OT, mts=mts: e.tensor_tensor(mts[:], OT[:], A["tz"][:], ALU.mult), reads=[RB[ob], A["r_tz"]], writes=[rm])
            P.dma(lambda e, mts=mts, q0=q0: e.dma_start(out=self.MT[layer][u * 128:(u + 1) * 128, q0:q0 + 512], in_=mts[:]),
                  reads=[rm], writes=[self.r_MT[layer][u][qt]])

    def layer_tail(self, layer):
        import contextlib
        nc, P = self.nc, self.P
        B, RB = self.banks, self.r_bank
        rc = self.r_const
        src = self.x if layer == 0 else self.H
        dst = self.H if layer == 0 else self.out
        r_dst = self.r_H if layer == 0 else self.r_out
        with contextlib.ExitStack() as es:
            def sb(name, shape, dt):
                return es.enter_context(nc.sbuf_tensor("T%d_%s" % (layer, name), list(shape), dt))
            wo = sb("wo", [128, KC, D], BF16)
            wg = sb("wg", [128, KC, D], BF16)
            wp = sb("wp", [128, 2, D], BF16)
            r_wo = [Res(), Res()]
            r_wg = [Res(), Res()]
            r_wp = Res()
            for half in range(2):
                P.dma(lambda e, half=half: e.dma_start(out=wo[:, half * 4:(half + 1) * 4, :],
                                                       in_=self.w_out[layer, half * 512:(half + 1) * 512, :].rearrange("(c p) n -> p c n", p=128)),
                      writes=[r_wo[half]], queue="pool")
            P.dma(lambda e: e.dma_start(out=wp[:], in_=self.w_ple[layer].rearrange("(c p) n -> p c n", p=128)), writes=[r_wp], queue="pool")
            for half in range(2):
                P.dma(lambda e, half=half: e.dma_start(out=wg[:, half * 4:(half + 1) * 4, :],
                                                       in_=self.w_gate[layer, half * 512:(half + 1) * 512, :].rearrange("(c p) n -> p c n", p=128)),
                      writes=[r_wg[half]], queue="pool")
            if layer == 0:
                self.mask_strip(sb, sb)
            mixg = [sb("mixg%d" % i, [128, KC, 512], BF16) for i in range(2)]
            r_mixg = [Res() for _ in range(2)]
            xts = [sb("xt%d" % i, [128, D], F32) for i in range(3)]
            r_xt = [Res() for _ in range(3)]
            pgs = [sb("pg%d" % i, [128, 4, 256], F32) for i in range(2)]
            r_pgs = [Res() for _ in range(2)]
            pbfs = [sb("pbf%d" % i, [128, 4, 256], BF16) for i in range(2)]
            r_pbfs = [Res() for _ in range(2)]
            pT = sb("pTall", [128, 2, S], BF16)
            r_pT = [Res() for _ in range(NG)]
            h1 = [sb("h1_%d" % i, [128, D], F32) for i in range(2)]
            r_h1 = [[Res(), Res()] for _ in range(2)]
            h1b = [sb("h1b%d" % i, [128, D], BF16) for i in range(2)]
            r_h1b = [Res() for _ in range(2)]
            h1T = [sb("h1T%d" % i, [128, D], BF16) for i in range(2)]
            r_h1T = [Res() for _ in range(2)]
            sg = [sb("sg%d" % i, [128, D], F32) for i in range(2)]
            r_sg = [[Res(), Res()] for _ in range(2)]
            tmp = sb("tmp", [128, D], F32)
            r_tmp = [Res(), Res()]
            h2 = [sb("h2_%d" % i, [128, D], F32) for i in range(2)]
            r_h2 = [Res() for _ in range(2)]
            b7 = B[7][:].bitcast(BF16)
            for tg in range(NG):
                pg, r_pg, pbf, r_pbf = pgs[tg % 2], r_pgs[tg % 2], pbfs[tg % 2], r_pbfs[tg % 2]
                P.dma(lambda e, tg=tg, pg=pg: e.dma_start(out=pg[:], in_=self.p[layer, tg * 512:(tg + 1) * 512, :].rearrange("(t p) f -> p t f", p=128)), writes=[r_pg])
                P.op("dve", lambda e, pg=pg, pbf=pbf: e.tensor_copy(pbf[:], pg[:]), reads=[r_pg], writes=[r_pbf])
                for t4 in range(4):
                    for c in range(2):
                        P.op("pe", lambda e, t4=t4, c=c, pbf=pbf: e.transpose(b7[:, c * 512 + t4 * 128:c * 512 + (t4 + 1) * 128], pbf[:, t4, c * 128:(c + 1) * 128], self.ident_bf[:]),
                             reads=[r_pbf, rc], writes=[RB[7]])
                P.op("act", lambda e, tg=tg: e.copy(pT[:, :, tg * 512:(tg + 1) * 512], b7.rearrange("p (c n) -> p c n", c=2)), reads=[RB[7]], writes=[r_pT[tg]])

            def load_group(tg):
                mg, rmg = mixg[tg % 2], r_mixg[tg % 2]
                P.dma(lambda e, mg=mg, tg=tg: e.dma_start(out=mg[:], in_=self.MT[layer][:, tg * 512:(tg + 1) * 512].rearrange("(c p) n -> p c n", p=128)),
                      reads=[self.r_MT[layer][c][tg] for c in range(8)], writes=[rmg])

            def load_x(t):
                xt, rx = xts[t % 3], r_xt[t % 3]
                P.dma(lambda e, xt=xt, t=t: e.dma_start(out=xt[:], in_=src[t * 128:(t + 1) * 128, :]), reads=([self.r_H[t]] if layer == 1 else []), writes=[rx])

            def s1(t):
                tg, t4 = divmod(t, 4)
                mg, rmg = mixg[tg % 2], r_mixg[tg % 2]
                xt, rx = xts[t % 3], r_xt[t % 3]
                i = t % 2
                for hf in range(2):
                    bk = 2 * i + hf
                    for c in range(KC):
                        P.op("pe", lambda e, bk=bk, hf=hf, c=c, mg=mg, t4=t4: e.matmul(B[bk][:, :], mg[:, c, t4 * 128:(t4 + 1) * 128], wo[:, c, hf * 512:(hf + 1) * 512],
                                                                                 start=(c == 0), stop=(c == KC - 1)),
                             reads=[rmg, r_wo[c // 4]], writes=[RB[bk]])
                    P.op("dve", lambda e, bk=bk, hf=hf, xt=xt, i=i: e.tensor_tensor(h1[i][:, hf * 512:(hf + 1) * 512], B[bk][:], xt[:, hf * 512:(hf + 1) * 512], ALU.add),
                         reads=[RB[bk], rx], writes=[r_h1[i][hf]])

            def s1c(t):
                i = t % 2
                P.op("act", lambda e, i=i: e.copy(h1b[i][:], h1[i][:]), reads=r_h1[i], writes=[r_h1b[i]])

            def s2(t):
                i = t % 2
                for c in range(KC):
                    P.op("pe", lambda e, c=c, i=i: e.transpose(b7[:, c * 128:(c + 1) * 128], h1b[i][:, c * 128:(c + 1) * 128], self.ident_bf[:]),
                         reads=[r_h1b[i], rc], writes=[RB[7]])
                P.op("act", lambda e, i=i: e.copy(h1T[i][:], b7), reads=[RB[7]], writes=[r_h1T[i]])

            def s3(t):
                i = t % 2
                for hf in range(2):
                    for c in range(KC):
                        P.op("pe", lambda e, hf=hf, c=c, i=i: e.matmul(B[4 + hf][:, :], h1T[i][:, c * 128:(c + 1) * 128], wg[:, c, hf * 512:(hf + 1) * 512],
                                                                   start=(c == 0), stop=(c == KC - 1)),
                             reads=[r_h1T[i], r_wg[c // 4]], writes=[RB[4 + hf]])
                    P.op("act", lambda e, hf=hf, i=i: e.activation(sg[i][:, hf * 512:(hf + 1) * 512], B[4 + hf][:], AF.Sigmoid), reads=[RB[4 + hf]], writes=[r_sg[i][hf]])

            def s4(t):
                i = t % 2
                tg = t // 4
                hh, rhh = h2[i], r_h2[i]
                for hf in range(2):
                    cs = slice(hf * 512, (hf + 1) * 512)
                    for c in range(2):
                        P.op("pe", lambda e, hf=hf, c=c, t=t: e.matmul(B[6][:, :], pT[:, c, t * 128:(t + 1) * 128], wp[:, c, hf * 512:(hf + 1) * 512],
                                                                   start=(c == 0), stop=(c == 1)),
                             reads=[r_pT[tg], r_wp], writes=[RB[6]])
                    P.op("dve", lambda e, cs=cs, i=i: e.tensor_tensor(tmp[:, cs], B[6][:], sg[i][:, cs], ALU.mult), reads=[RB[6], r_sg[i][hf]], writes=[r_tmp[hf]])
                P.op("dve", lambda e, hh=hh, i=i: e.tensor_tensor(hh[:], tmp[:], h1[i][:], ALU.add), reads=r_tmp + r_h1[i], writes=[rhh])
                P.dma(lambda e, hh=hh, t=t: e.dma_start(out=dst[t * 128:(t + 1) * 128, :], in_=hh[:]), reads=[rhh], writes=[r_dst[t]], queue="pool")

            load_group(0)
            load_group(1)
            load_x(0)
            load_x(1)
            s1(0)
            s1c(0)
            for t in range(NT):
                if t + 2 < NT:
                    load_x(t + 2)
                if t % 4 == 0 and t >= 4 and t // 4 + 1 < NG:
                    load_group(t // 4 + 1)
                if t + 1 < NT:
                    s1(t + 1)
                s2(t)
                s3(t)
                if t + 1 < NT:
                    s1c(t + 1)
                s4(t)


def _unit_cols(layer):
    cols = []
    for u in range(8):
        if layer == 0:
            if u < 4:
                q, k, v, z = 128 * u, 512 + 128 * u, 1024 + 128 * u, 3076 + 128 * u
            else:
                hh = u - 4
                q, k, v, z = 1536 + 128 * hh, 2048 + 128 * hh, 2560 + 128 * hh, 3076 + 512 + 128 * hh
        else:
            q, k, v, z = 128 * u, 1024 + 128 * u, 2048 + 128 * u, 3072 + 128 * u
        for s in (q, k, v, z):
            cols.extend(range(s, s + 128))
    return np.asarray(cols)


def _prep_inputs(inp):
    f = lambda a: np.ascontiguousarray(np.asarray(a, dtype=np.float32))
    w_in_even = f(inp["w_in_even"])[0]
    w_in_odd = f(inp["w_in_odd"])[0]
    w_in0 = np.ascontiguousarray(w_in_even[:, _unit_cols(0)])
    w_in1 = np.ascontiguousarray(w_in_odd[:, _unit_cols(1)])
    w_f = np.zeros((D, 128), np.float32)
    cols = np.zeros((128, 4), np.float32)
    cols[:, 0] = f(inp["subln_g"])[0]
    bfg = f(inp["b_forget"])[0]
    for h in range(4):
        for r in (h, 32 + h, 64 + h):
            w_f[:, r] = w_in_even[:, 3072 + h]
            cols[r, 1] = bfg[h]
    sm = np.zeros((1, SM_N), np.float32)
    ng = f(inp["norm_g"])
    sm[0, SM_G0:SM_G0 + D] = ng[0]
    sm[0, SM_G1:SM_G1 + D] = ng[1]
    qa, ka = f(inp["qn_a"])[0], f(inp["kn_a"])[0]
    sm[0, SM_GA:SM_GA + 256] = np.concatenate([qa, qa, ka, ka])
    sm[0, SM_GB:SM_GB + 256] = np.concatenate([f(inp["qn_b"])[0], f(inp["kn_b"])[0]])
    sm[0, SM_GC:SM_GC + 256] = np.concatenate([f(inp["qn_c"])[0], f(inp["kn_c"])[0]])
    sm[0, SM_LAM:SM_LAM + 256] = np.concatenate([f(inp["lam_q1"])[0], f(inp["lam_k1"])[0], f(inp["lam_q2"])[0], f(inp["lam_k2"])[0]])
    shared = dict(smalls=sm, cols=cols, w_in0=w_in0, w_f=w_f, w_in1=w_in1,
                  w_out=np.ascontiguousarray(np.stack([f(inp["w_out_even"])[0], f(inp["w_out_odd"])[0]])),
                  w_ple=f(inp["w_ple"]), w_gate=f(inp["w_ple_gate"]))
    x = f(inp["x"])
    p = f(inp["p"])
    pos = np.asarray(inp["positions"]).astype(np.int32)
    maps = []
    for b in range(8):
        m = dict(shared)
        m["x"] = x[b]
        m["p"] = np.ascontiguousarray(p[:, b])
        m["pos"] = np.ascontiguousarray(pos[b].reshape(NT, 128).T)
        maps.append(m)
    return maps


def kernel(**inputs):
    maps = _prep_inputs(inputs)
    nc = Builder().build()
    res = run_bass_kernel_spmd(nc, maps, core_ids=list(range(8)))
    return np.stack([np.asarray(r["out"], dtype=np.float32) for r in res.results], axis=0)
```
